# Optimizing a Trainium2 kernel written in Bass

```python
import jax, jax.numpy as jnp
from jax import lax
import numpy as np


D_MODEL = 2048
BATCH = 8
SEQ = 4096
DEPTH = 2
DEC_BATCH = 1
DEC_SEQ = 16384
PAST_LEN = 128

N_GROUPS = 4
GROUP_DIM = 128
MIX_W = N_GROUPS * GROUP_DIM
POOL_WINDOWS = (2, 4, 8, 16)
GDN_HEADS = 4
GDN_DK = 128
GDN_DV = 128
GDN_CONV = 4
GDN_CHUNK = 64
CONF_WIDTH = 31
SGU_CHUNK = 128
D_FF = 4 * D_MODEL
N_BRANCH = 4
EPS = 1e-6

D_POOL_IN = MIX_W
D_GDN_QKV = 3 * MIX_W
D_GDN_Z = MIX_W
D_GDN_AB = 4 * GDN_HEADS
D_CONF_IN = 2 * MIX_W
D_SGU_IN = 2 * MIX_W
D_GATE = N_BRANCH * D_MODEL
D_IN = D_POOL_IN + D_GDN_QKV + D_GDN_Z + D_GDN_AB + D_CONF_IN + D_SGU_IN + D_GATE
SPLITS = (D_POOL_IN,
          D_POOL_IN + D_GDN_QKV,
          D_POOL_IN + D_GDN_QKV + D_GDN_Z,
          D_POOL_IN + D_GDN_QKV + D_GDN_Z + D_GDN_AB,
          D_POOL_IN + D_GDN_QKV + D_GDN_Z + D_GDN_AB + D_CONF_IN,
          D_POOL_IN + D_GDN_QKV + D_GDN_Z + D_GDN_AB + D_CONF_IN + D_SGU_IN)

kernel_name = 'hybrid_bidir_encoder'


def rmsnorm(x, w):
    x32 = x.astype(jnp.float32)
    y = x32 * lax.rsqrt(jnp.mean(x32 * x32, axis=-1, keepdims=True) + EPS)
    return (y * w.astype(jnp.float32)).astype(x.dtype)


def layernorm(x, w, b):
    x32 = x.astype(jnp.float32)
    mu = jnp.mean(x32, axis=-1, keepdims=True)
    var = jnp.mean(jnp.square(x32 - mu), axis=-1, keepdims=True)
    y = (x32 - mu) * lax.rsqrt(var + EPS)
    return (y * w.astype(jnp.float32) + b.astype(jnp.float32)).astype(x.dtype)


def l2norm(x):
    return x * lax.rsqrt(jnp.sum(x * x, axis=-1, keepdims=True) + EPS)


def depthwise_conv(x, w):
    k, c = w.shape
    pad_l = (k - 1) // 2
    return lax.conv_general_dilated(x, w[:, None, :].astype(x.dtype), (1,), [(pad_l, k - 1 - pad_l)],
                                    dimension_numbers=('NWC', 'WIO', 'NWC'), feature_group_count=c)


def pool_mixer(xa, pool_w, pool_scale):
    b, s, _ = xa.shape
    xg = xa.astype(jnp.float32).reshape(b, s, N_GROUPS, GROUP_DIM)
    cs = jnp.pad(jnp.cumsum(xg, axis=1), ((0, 0), (1, 0), (0, 0), (0, 0)))
    win = jnp.array(POOL_WINDOWS, jnp.int32)[None, :]
    t = jnp.arange(s, dtype=jnp.int32)[:, None]
    lo = jnp.clip(t - win // 2, 0, s)
    hi = jnp.clip(t - win // 2 + win, 0, s)
    grp = jnp.arange(N_GROUPS, dtype=jnp.int32)[None, :]
    cnt = (hi - lo).astype(jnp.float32)[None, :, :, None]
    mean = (cs[:, hi, grp] - cs[:, lo, grp]) / cnt
    y = jnp.einsum('bsgc,gcd->bsgd', mean - xg, pool_w.astype(jnp.float32))
    return (y.reshape(b, s, MIX_W) * pool_scale.astype(jnp.float32)).astype(xa.dtype)


def chunk_gated_delta(q, k, v, g, beta):
    b, s, h, dk = q.shape
    dv = v.shape[-1]
    n = s // GDN_CHUNK
    q = l2norm(q) * (dk ** -0.5)
    k = l2norm(k)
    q, k, v = [t.transpose(0, 2, 1, 3).reshape(b, h, n, GDN_CHUNK, -1) for t in (q, k, v)]
    g = g.transpose(0, 2, 1).reshape(b, h, n, GDN_CHUNK)
    beta = beta.transpose(0, 2, 1).reshape(b, h, n, GDN_CHUNK)
    gc = jnp.cumsum(g, axis=-1)
    tri = jnp.tril(jnp.ones((GDN_CHUNK, GDN_CHUNK), bool))
    strict = jnp.tril(jnp.ones((GDN_CHUNK, GDN_CHUNK), bool), -1)
    decay = jnp.exp(jnp.where(tri, gc[..., :, None] - gc[..., None, :], -jnp.inf))
    kb = k * beta[..., None]
    vb = v * beta[..., None]
    lmat = jnp.where(strict, jnp.einsum('bhnid,bhnjd->bhnij', kb, k) * decay, 0.0)
    u = lax.linalg.triangular_solve(lmat, vb, left_side=True, lower=True, unit_diagonal=True)
    w = lax.linalg.triangular_solve(lmat, kb * jnp.exp(gc)[..., None], left_side=True, lower=True,
                                    unit_diagonal=True)
    qk = jnp.einsum('bhnid,bhnjd->bhnij', q, k) * decay

    def step(state, inp):
        qn, kn, un, wn, gn, qkn = inp
        v_new = un - jnp.einsum('bhcd,bhde->bhce', wn, state)
        o = (jnp.einsum('bhcd,bhde->bhce', qn * jnp.exp(gn)[..., None], state)
             + jnp.einsum('bhij,bhje->bhie', qkn, v_new))
        g_last = gn[..., -1]
        state = (state * jnp.exp(g_last)[..., None, None]
                 + jnp.einsum('bhcd,bhce->bhde', kn * jnp.exp(g_last[..., None] - gn)[..., None], v_new))
        return state, o

    xs = tuple(jnp.moveaxis(t, 2, 0) for t in (q, k, u, w, gc, qk))
    state0 = jnp.zeros((b, h, dk, dv), jnp.float32)
    _, o = lax.scan(step, state0, xs)
    return o.transpose(1, 0, 3, 2, 4).reshape(b, s, h, dv)


def gdn_mixer(xqkv, xz, xab, conv_w, a_log, dt_bias, norm_w):
    b, s, _ = xqkv.shape
    qkv = jax.nn.silu(depthwise_conv(xqkv, conv_w)).astype(jnp.float32)
    q, k, v = jnp.split(qkv, 3, axis=-1)
    q = q.reshape(b, s, GDN_HEADS, GDN_DK)
    k = k.reshape(b, s, GDN_HEADS, GDN_DK)
    v = v.reshape(b, s, GDN_HEADS, GDN_DV)
    ab = xab.astype(jnp.float32).reshape(b, s, 2, 2, GDN_HEADS)
    beta = jax.nn.sigmoid(ab[:, :, 0])
    g = -jnp.exp(a_log.astype(jnp.float32)) * jax.nn.softplus(ab[:, :, 1] + dt_bias.astype(jnp.float32))
    o_f = chunk_gated_delta(q, k, v, g[:, :, 0], beta[:, :, 0])
    o_b = chunk_gated_delta(q[:, ::-1], k[:, ::-1], v[:, ::-1], g[:, ::-1, 1], beta[:, ::-1, 1])[:, ::-1]
    o = o_f + o_b
    o = o * lax.rsqrt(jnp.mean(o * o, axis=-1, keepdims=True) + EPS) * norm_w.astype(jnp.float32)
    o = o * jax.nn.silu(xz.astype(jnp.float32).reshape(b, s, GDN_HEADS, GDN_DV))
    return o.reshape(b, s, MIX_W).astype(xqkv.dtype)


def conformer_conv(xc, dw_w, dw_b, ln_w, ln_b):
    a, gt = jnp.split(xc, 2, axis=-1)
    h = a * jax.nn.sigmoid(gt)
    h = depthwise_conv(h, dw_w) + dw_b
    return jax.nn.silu(layernorm(h, ln_w, ln_b))


def spatial_gating(xd, ln_w, ln_b, w_s, b_s):
    b, s, _ = xd.shape
    u, v = jnp.split(jax.nn.gelu(xd), 2, axis=-1)
    v = layernorm(v, ln_w, ln_b).reshape(b, s // SGU_CHUNK, SGU_CHUNK, N_GROUPS, GROUP_DIM)
    sv = jnp.einsum('gij,bnjgc->bnigc', w_s, v) + b_s.T[:, :, None]
    return u * sv.reshape(b, s, MIX_W)


def encoder(x, norm_mix_w, w_in, pool_w, pool_scale, pool_proj, gdn_conv_w, gdn_a_log, gdn_dt_bias,
            gdn_norm_w, gdn_proj, conf_dw_w, conf_dw_b, conf_ln_w, conf_ln_b, conf_proj, sgu_ln_w, sgu_ln_b,
            sgu_w, sgu_b, sgu_proj, w_out, norm_mlp_w, mlp_w1, mlp_w2, norm_final_w):
    b, s, _ = x.shape
    for l in range(DEPTH):
        h = rmsnorm(x, norm_mix_w[l])
        xa, xqkv, xz, xab, xc, xd, xg = jnp.split(h @ w_in[l], SPLITS, axis=-1)
        ya = pool_mixer(xa, pool_w[l], pool_scale[l]) @ pool_proj[l]
        yb = gdn_mixer(xqkv, xz, xab, gdn_conv_w[l], gdn_a_log[l], gdn_dt_bias[l], gdn_norm_w[l]) @ gdn_proj[l]
        yc = conformer_conv(xc, conf_dw_w[l], conf_dw_b[l], conf_ln_w[l], conf_ln_b[l]) @ conf_proj[l]
        yd = spatial_gating(xd, sgu_ln_w[l], sgu_ln_b[l], sgu_w[l], sgu_b[l]) @ sgu_proj[l]
        gate = jax.nn.sigmoid(xg.astype(jnp.float32)).astype(x.dtype).reshape(b, s, N_BRANCH, D_MODEL)
        merged = gate[:, :, 0] * ya + gate[:, :, 1] * yb + gate[:, :, 2] * yc + gate[:, :, 3] * yd
        x = x + merged @ w_out[l]
        h = rmsnorm(x, norm_mlp_w[l])
        x = x + jnp.square(jax.nn.relu(h @ mlp_w1[l])) @ mlp_w2[l]
    return rmsnorm(x, norm_final_w)


def setup_inputs(seed: int = 0) -> dict:
    key = jax.random.key(seed)
    ks = jax.random.split(key, 32)

    def nrm(k, shape, scale):
        return jax.random.normal(k, shape, jnp.float32) * scale

    def gain(k, shape):
        return 1.0 + 0.1 * jax.random.normal(k, shape, jnp.float32)

    dt = jnp.exp(jax.random.uniform(ks[9], (DEPTH, 2, GDN_HEADS), jnp.float32,
                                    minval=float(np.log(1e-3)), maxval=float(np.log(1e-1))))
    return {
        'x_prompt': nrm(ks[0], (BATCH, SEQ, D_MODEL), 1.0),
        'x_sample': nrm(ks[1], (DEC_BATCH, DEC_SEQ, D_MODEL), 1.0),
        'norm_mix_w': gain(ks[2], (DEPTH, D_MODEL)),
        'w_in': nrm(ks[3], (DEPTH, D_MODEL, D_IN), D_MODEL ** -0.5),
        'pool_w': nrm(ks[4], (DEPTH, N_GROUPS, GROUP_DIM, GROUP_DIM), GROUP_DIM ** -0.5),
        'pool_scale': gain(ks[5], (DEPTH, MIX_W)),
        'pool_proj': nrm(ks[6], (DEPTH, MIX_W, D_MODEL), MIX_W ** -0.5),
        'gdn_conv_w': nrm(ks[7], (DEPTH, GDN_CONV, D_GDN_QKV), GDN_CONV ** -0.5),
        'gdn_a_log': jnp.log(jax.random.uniform(ks[8], (DEPTH, 2, GDN_HEADS), jnp.float32, minval=1.0, maxval=16.0)),
        'gdn_dt_bias': jnp.log(jnp.expm1(dt)),
        'gdn_norm_w': gain(ks[10], (DEPTH, GDN_DV)),
        'gdn_proj': nrm(ks[11], (DEPTH, MIX_W, D_MODEL), MIX_W ** -0.5),
        'conf_dw_w': nrm(ks[12], (DEPTH, CONF_WIDTH, MIX_W), CONF_WIDTH ** -0.5),
        'conf_dw_b': nrm(ks[13], (DEPTH, MIX_W), 0.02),
        'conf_ln_w': gain(ks[14], (DEPTH, MIX_W)),
        'conf_ln_b': nrm(ks[15], (DEPTH, MIX_W), 0.02),
        'conf_proj': nrm(ks[16], (DEPTH, MIX_W, D_MODEL), MIX_W ** -0.5),
        'sgu_ln_w': gain(ks[17], (DEPTH, MIX_W)),
        'sgu_ln_b': nrm(ks[18], (DEPTH, MIX_W), 0.02),
        'sgu_w': nrm(ks[19], (DEPTH, N_GROUPS, SGU_CHUNK, SGU_CHUNK), SGU_CHUNK ** -0.5),
        'sgu_b': gain(ks[20], (DEPTH, N_GROUPS, SGU_CHUNK)),
        'sgu_proj': nrm(ks[21], (DEPTH, MIX_W, D_MODEL), MIX_W ** -0.5),
        'w_out': nrm(ks[22], (DEPTH, D_MODEL, D_MODEL), D_MODEL ** -0.5),
        'norm_mlp_w': gain(ks[23], (DEPTH, D_MODEL)),
        'mlp_w1': nrm(ks[24], (DEPTH, D_MODEL, D_FF), D_MODEL ** -0.5),
        'mlp_w2': nrm(ks[25], (DEPTH, D_FF, D_MODEL), D_FF ** -0.5),
        'norm_final_w': gain(ks[26], (D_MODEL,)),
    }


def reference(x_prompt, x_sample, norm_mix_w, w_in, pool_w, pool_scale, pool_proj, gdn_conv_w, gdn_a_log,
              gdn_dt_bias, gdn_norm_w, gdn_proj, conf_dw_w, conf_dw_b, conf_ln_w, conf_ln_b, conf_proj,
              sgu_ln_w, sgu_ln_b, sgu_w, sgu_b, sgu_proj, w_out, norm_mlp_w, mlp_w1, mlp_w2, norm_final_w):
    params = (norm_mix_w, w_in, pool_w, pool_scale, pool_proj, gdn_conv_w, gdn_a_log, gdn_dt_bias, gdn_norm_w,
              gdn_proj, conf_dw_w, conf_dw_b, conf_ln_w, conf_ln_b, conf_proj, sgu_ln_w, sgu_ln_b, sgu_w, sgu_b,
              sgu_proj, w_out, norm_mlp_w, mlp_w1, mlp_w2, norm_final_w)
    y_prompt = encoder(x_prompt, *params)
    y_sample = encoder(x_sample, *params)
    return (y_prompt, y_sample)
```

```python
import contextlib
import numpy as np
import concourse.bass as bass
import concourse.mybir as mybir
from concourse.bass_utils import run_bass_kernel_spmd

F32 = mybir.dt.float32
BF16 = mybir.dt.bfloat16
ALU = mybir.AluOpType
AF = mybir.ActivationFunctionType

D = 2048
NC16 = 16
T = 512
HW = 16
W = T + 2 * HW
EPS = 1e-6
NBLK = 61
POOL_WINDOWS = (2, 4, 8, 16)
NEG = -30000.0


class Buf:
    __slots__ = ("ap", "w", "r", "excl")

    def __init__(self, ap, excl=False):
        self.ap = ap
        self.w = None
        self.r = {}
        self.excl = excl


class Sub:
    def __init__(self, parent, ap):
        self.ap = ap
        self.p = parent

    @property
    def excl(self):
        return self.p.excl

    @property
    def w(self):
        return self.p.w

    @w.setter
    def w(self, v):
        self.p.w = v

    @property
    def r(self):
        return self.p.r

    @r.setter
    def r(self, v):
        self.p.r = v


class DSem:
    def __init__(self, sem):
        self.sem = sem
        self.cnt = 0


class Prog:
    ENGS = ("pe", "act", "dve", "pool", "sp")

    def __init__(self, nc, stack):
        self.nc = nc
        self.streams = {e: [] for e in self.ENGS}
        self.cnt = {e: 0 for e in self.ENGS}
        self.sem = {e: stack.enter_context(nc.semaphore("s_" + e)) for e in self.ENGS}
        self.seen = {e: {} for e in self.ENGS}
        self.nops = 0

    def _wait(self, e, evs):
        own = self.sem[e]
        seen = self.seen[e]
        for (sem, val) in evs:
            if sem is own and e == "pe":
                continue
            k = id(sem)
            if seen.get(k, 0) >= val:
                continue
            seen[k] = val
            self.streams[e].append(lambda eng, sem=sem, val=val: eng.wait_ge(sem, val))

    @staticmethod
    def _collect(reads, writes):
        evs = []
        for b in reads:
            if b.w is not None:
                evs.append(b.w)
            if b.excl:
                evs.extend(b.r.values())
        for b in writes:
            if b.w is not None:
                evs.append(b.w)
            evs.extend(b.r.values())
        return evs

    def op(self, e, fn, reads=(), writes=(), inc=True):
        self._wait(e, self._collect(reads, writes))
        sem = self.sem[e]
        nv = self.cnt[e] + 1
        ev = (sem, nv)
        if inc:
            self.cnt[e] = nv
            self.streams[e].append(lambda eng, f=fn, sem=sem: f(eng).then_inc(sem, 1))
        else:
            self.streams[e].append(lambda eng, f=fn: f(eng))
        k = id(sem)
        for b in reads:
            b.r[k] = ev
        for b in writes:
            b.w = ev
            b.r = {}
        self.nops += 1

    def dma(self, q, out_ap, in_ap, ds, reads=(), writes=()):
        self._wait(q, self._collect(reads, writes))
        ds.cnt += 16
        ev = (ds.sem, ds.cnt)
        self.streams[q].append(lambda eng, o=out_ap, i=in_ap, s=ds.sem: eng.dma_start(out=o, in_=i).then_inc(s, 16))
        k = id(ds.sem)
        for b in reads:
            b.r[k] = ev
        for b in writes:
            b.w = ev
            b.r = {}
        self.nops += 1

    def retag(self, bufs, ds):
        for b in bufs:
            b.w = (ds.sem, ds.cnt)

    def wait_ev(self, e, ev):
        self._wait(e, [ev])

    def fence(self):
        snap = [(self.sem[e], self.cnt[e]) for e in self.ENGS if self.cnt[e] > 0]
        for e in self.ENGS:
            self._wait(e, snap)


class _Stop(Exception):
    pass


STOP = [99]


def _ck(n):
    if STOP[0] <= n:
        raise _Stop()


def build_program(seqs, depth, n_out_seq=None):
    nc = bass.Bass("TRN2", target_bir_lowering=False)
    stack = contextlib.ExitStack()
    nseq = len(seqs)
    xin = [nc.dram_tensor(f"x{i}", [D, L], F32, kind="ExternalInput").ap() for i, L in enumerate(seqs)]
    yout = [nc.dram_tensor(f"y{i}", [D, L], F32, kind="ExternalOutput").ap() for i, L in enumerate(seqs)]
    wb = nc.dram_tensor("wb", [depth * NBLK, 128, 8192], F32, kind="ExternalInput").ap()
    cpp = nc.dram_tensor("cpp", [depth, 128, 512], F32, kind="ExternalInput").ap()
    crow = nc.dram_tensor("crow", [depth, 128, 2048], F32, kind="ExternalInput").ap()
    cw = nc.dram_tensor("cw", [depth, 128, 1280], F32, kind="ExternalInput").ap()
    cmask = nc.dram_tensor("cmask", [128, 1024], F32, kind="ExternalInput").ap()
    cfin = nc.dram_tensor("cfin", [128, 16], F32, kind="ExternalInput").ap()
    wbf = nc.dram_tensor("wbf", [depth * NBLK, 128, 8192], BF16).ap()
    wp = nc.dram_tensor("wp", [depth * 16, 128, 2048], F32, kind="ExternalInput").ap()
    wpbf = nc.dram_tensor("wpbf", [depth * 16, 128, 2048], BF16).ap()
    xs = [[xin[i]] + [nc.dram_tensor(f"xs{l}_{i}", [D, L], F32).ap() for l in range(1, depth)] for i, L in enumerate(seqs)]
    obs = [nc.dram_tensor(f"ob{i}", [512, L], F32).ap() for i, L in enumerate(seqs)]

    P = Prog(nc, stack)

    def sb(name, shape, dt=F32):
        return stack.enter_context(nc.sbuf_tensor(name, shape, dt))

    def ps(name, shape, dt=F32):
        return stack.enter_context(nc.psum_tensor(name, shape, dt))

    def dsem(name):
        return DSem(stack.enter_context(nc.semaphore(name)))

    xw = Buf(sb("xw", [128, NC16, W]))
    h = Buf(sb("h", [128, NC16, W], BF16))
    wsl_t = [sb(f"wsl{i}", [128, 8192], BF16) for i in range(2)]
    wsl = [Buf(t) for t in wsl_t]
    ARENA_B = 50688
    arena = sb("arena", [128, ARENA_B // 4])
    a_f32 = arena
    qkvw = Buf(a_f32[:, 0:4 * W].rearrange("p (c w) -> p c w", c=4))
    qkv = Buf(a_f32[:, 4 * W:4 * W + 12 * T].rearrange("p (c w) -> p c w", c=12))
    o_q = 4 * W + 12 * T
    qkn_t = a_f32[:, o_q:o_q + 8 * T // 2].bitcast(BF16).rearrange("p (c w) -> p c w", c=8)
    qkn = Buf(qkn_t)
    o_o = o_q + 8 * T // 2
    obT = Buf(a_f32[:, o_o:o_o + 4 * T].rearrange("p (c w) -> p c w", c=4))
    assert (o_o + 4 * T) * 4 <= ARENA_B
    bsA = Buf(a_f32[:, 0:4 * W].rearrange("p (c w) -> p c w", c=4))
    bsB = Buf(a_f32[:, 4 * W:8 * W].rearrange("p (c w) -> p c w", c=4))
    mbf = Buf(a_f32[:, 0:NC16 * T // 2].bitcast(BF16).rearrange("p (c w) -> p c w", c=NC16))
    uq = [Buf(a_f32[:, i * 4096:(i + 1) * 4096].bitcast(BF16).rearrange("p (c w) -> p c w", c=NC16)) for i in range(2)]
    yst = Buf(a_f32[:, 0:NC16 * T // 2 * 0 + 8192].rearrange("p (c w) -> p c w", c=NC16))
    def _abin(i):
        o = 6464 + i * 1024
        return Buf(a_f32[:, o:o + 1024].bitcast(BF16).rearrange("p (c w) -> p c w", c=4))
    binb = [_abin(0), Buf(sb("bin1", [128, 4, T], BF16)), _abin(1), _abin(2)]
    accb = [Buf(a_f32[:, 4096 + i * 512:4096 + (i + 1) * 512]) for i in range(4)]
    sq = [Buf(sb(f"sq{i}", [128, W], BF16)) for i in range(2)]
    rstd = Buf(sb("rstd", [128, W]))
    tmp = [Buf(sb(f"tmp{i}", [128, T])) for i in range(4)]
    psl = [Buf(sb(f"psl{i}", [128, 2048], BF16)) for i in range(2)]
    zs = Buf(sb("zs", [128, 4, T], BF16))
    stmp = [Buf(a_f32[:, 4352 + i * W:4352 + (i + 1) * W]) for i in range(2)]
    dbf = Buf(sb("dbf", [128, T], BF16))
    vg = Buf(a_f32[:, 5440:5952])
    sqv = Buf(a_f32[:, 5952:6464])
    vln = [Buf(sb(f"vln{i}", [128, T], BF16)) for i in range(2)]
    st4 = Buf(sb("st4", [128, 8]))
    c_pp = Buf(sb("c_pp", [128, 512]))
    c_row = Buf(sb("c_row", [128, 2048]))
    c_w = Buf(a_f32[:, 0:1280])
    c_wb = Buf(sb("c_wb", [128, 1280], BF16))
    c_mask = Buf(sb("c_mask", [128, 1024]))
    c_fin = Buf(sb("c_fin", [128, 16]))
    ones_b = Buf(sb("ones_b", [128, 128], BF16))
    ident_b = Buf(sb("ident_b", [128, 128], BF16))
    abt = Buf(sb("abt", [64, 8, 16]))
    beta = Buf(sb("beta", [64, 8, 8]))
    gg = Buf(sb("gg", [64, 8, 8]))
    gcs = Buf(sb("gcs", [64, 32]))
    gtot = Buf(sb("gtot", [128, 32]))
    glast = Buf(sb("glast", [128, 32]))
    dl = Buf(sb("dl", [64, 32]))
    HS = 4
    gbc = [Buf(sb(f"gbc{i}", [64, 128])) for i in range(HS)]
    grow = [Buf(sb(f"grow{i}", [128, 64])) for i in range(HS)]
    gkb = [Buf(sb(f"gkb{i}", [128, 64], BF16)) for i in range(HS)]
    gqb = [Buf(sb(f"gqb{i}", [128, 64], BF16)) for i in range(HS)]
    dtm = [Buf(sb(f"dtm{i}", [64, 64])) for i in range(HS)]
    Dt = [Buf(sb(f"Dt{i}", [64, 64])) for i in range(HS)]
    Dts = [Buf(sb(f"Dts{i}", [64, 64])) for i in range(HS)]
    Ptb = [Buf(sb(f"Ptb{i}", [64, 64], BF16)) for i in range(HS)]
    PY = [[Buf(sb(f"PY{i}_{j}", [64, 128], BF16)) for j in range(2)] for i in range(HS)]
    PTt = [[Buf(sb(f"PT{i}_{j}", [64, 64], BF16)) for j in range(2)] for i in range(HS)]
    Kd = [Buf(sb(f"Kd{i}", [64, 128], BF16)) for i in range(HS)]
    Vt = [Buf(sb(f"Vt{i}", [64, 128])) for i in range(HS)]
    Rp = [Buf(sb(f"Rp{i}", [64, 128], BF16)) for i in range(HS)]
    Vn = [Buf(sb(f"Vn{i}", [64, 128], BF16)) for i in range(HS)]
    S32 = [Buf(sb(f"S32_{i}", [128, 128])) for i in range(4)]
    Sbf = [Buf(sb(f"Sbf_{i}", [128, 128], BF16)) for i in range(4)]
    pa = Buf(ps("pa", [128, 512]), excl=True)
    pb = Buf(ps("pb", [128, 512]), excl=True)
    pn = Buf(ps("pn", [128, 512]), excl=True)
    pg_t = [ps(f"pg{i}", [128, 512]) for i in range(4)]
    pgP = [Buf(t, excl=True) for t in pg_t]
    pgA = [Sub(pgP[i], t[:, 0:64]) for i, t in enumerate(pg_t)]
    pgB = [Sub(pgP[i], t[:, 64:192]) for i, t in enumerate(pg_t)]
    pgC = [Sub(pgP[i], t[:, 192:320]) for i, t in enumerate(pg_t)]
    pgD = [Sub(pgP[i], t[:, 320:384]) for i, t in enumerate(pg_t)]
    pgE = [Sub(pgP[i], t[:, 384:512]) for i, t in enumerate(pg_t)]
    pt_t = ps("pt", [128, 1024], BF16)
    ptP = Buf(pt_t, excl=True)
    ptX = [Sub(ptP, pt_t[:, i * 256:i * 256 + 64]) for i in range(4)]
    ptK = [Sub(ptP, pt_t[:, i * 256 + 64:i * 256 + 192]) for i in range(4)]
    pab = [pa, pb]
    pab_i = [0]

    def nextp():
        pab_i[0] ^= 1
        return pab[pab_i[0]]

    ds_x = dsem("ds_x")
    ds_w = [dsem("ds_w0"), dsem("ds_w1")]
    ds_c = dsem("ds_c")
    ds_cast = [dsem(f"ds_cast{i}") for i in range(8)]
    ds_ob = dsem("ds_ob")
    ds_y = dsem("ds_y")
    ds_p = [dsem("ds_p0"), dsem("ds_p1")]
    ds_castp = dsem("ds_castp")
    wb_bufs = [Buf(None) for _ in range(depth * NBLK)]
    wp_bufs = [Buf(None) for _ in range(depth * 16)]
    xs_bufs = [[[Buf(None) for _ in range(L // T)] for _ in range(depth)] for L in seqs]
    ob_bufs = [[Buf(None) for _ in range(L // T)] for L in seqs]

    def mm(out, lhsT, rhs, start, stop, reads, writes, inc=True):
        P.op("pe", lambda e: e.matmul(out, lhsT, rhs, start=start, stop=stop), reads, writes, inc)

    def tr(out, in_, ident, reads, writes):
        P.op("pe", lambda e: e.transpose(out, in_, ident), reads, writes)

    def act(out, in_, func, reads, writes, **kw):
        P.op("act", lambda e: e.activation(out=out, in_=in_, func=func, **kw), reads, writes)

    def tt(eng, out, in0, in1, op, reads, writes):
        P.op(eng, lambda e: e.tensor_tensor(out=out, in0=in0, in1=in1, op=op), reads, writes)

    def tsc(eng, out, in0, s1, s2, op0, op1, reads, writes):
        if s2 is None:
            P.op(eng, lambda e: e.tensor_scalar(out=out, in0=in0, scalar1=s1, scalar2=None, op0=op0), reads, writes)
        else:
            P.op(eng, lambda e: e.tensor_scalar(out=out, in0=in0, scalar1=s1, scalar2=s2, op0=op0, op1=op1), reads, writes)

    def stt(out, in0, scalar, in1, op0, op1, reads, writes):
        P.op("dve", lambda e: e.scalar_tensor_tensor(out=out, in0=in0, scalar=scalar, in1=in1, op0=op0, op1=op1), reads, writes)

    def rsq(out, in_, scale, reads, wbuf):
        act(out, in_, AF.Sqrt, reads, [wbuf], scale=scale, bias=EPS)

    def recip(ap, buf):
        P.op("dve", lambda e: e.reciprocal(ap, ap), [buf], [buf])

    def cp(eng, out, in_, reads, writes):
        if eng == "act":
            act(out, in_, AF.Copy, reads, writes)
        else:
            P.op(eng, lambda e: e.tensor_copy(out=out, in_=in_), reads, writes)

    def mset(eng, ap, val, writes):
        P.op(eng, lambda e: e.memset(ap, val), (), writes)

    mset("pool", ones_b.ap[:], 1.0, [ones_b])
    P.dma("sp", c_mask.ap[:], cmask[:, :], ds_c, (), [c_mask])
    P.dma("sp", c_fin.ap[:], cfin[:, :], ds_c, (), [c_fin])
    P.retag([c_mask, c_fin], ds_c)
    cp("dve", ident_b.ap[:], c_mask.ap[:, 0:128], [c_mask], [ident_b])
    ident_f = c_mask.ap[:, 0:128]
    ones_f = c_mask.ap[:, 896:1024]

    def MASK(d, k):
        o = 128 + (d * 3 + k) * 64
        return c_mask.ap[0:64, o:o + 64]
    ident64_f = c_mask.ap[0:64, 0:64]

    ncast = [0]

    def cast_dma(dst, src_, buf):
        dsx = ds_cast[ncast[0] % 8]
        if dsx.cnt > 0:
            P.wait_ev("pool", (dsx.sem, dsx.cnt))
        P.dma("pool", dst, src_, dsx, (), [buf])
        ncast[0] += 1

    for i in range(depth * NBLK):
        cast_dma(wbf[i], wb[i], wb_bufs[i])

    for i in range(depth * 16):
        cast_dma(wpbf[i], wp[i], wp_bufs[i])
    class WS:
        order = []
        pos = 0
        issued = 0

    def ws_next(bid_check):
        while WS.issued < min(WS.pos + 2, len(WS.order)):
            gb = WS.order[WS.issued]
            s = WS.issued % 2
            P.dma("sp", wsl[s].ap[:, :], wbf[gb], ds_w[s], [wb_bufs[gb]], [wsl[s]])
            WS.issued += 1
        assert WS.order[WS.pos] == bid_check, (WS.pos, WS.order[WS.pos], bid_check)
        s = WS.pos % 2
        WS.pos += 1
        return wsl[s]

    def kview(slot):
        return slot.ap[:, :].rearrange("p (k c) -> p k c", k=16)

    def load_layer_consts(l):
        P.dma("sp", c_pp.ap[:], cpp[l], ds_c, (), [c_pp])
        P.dma("sp", c_row.ap[:], crow[l], ds_c, (), [c_row])
        P.dma("sp", c_w.ap[:], cw[l], ds_c, (), [c_w])
        P.retag([c_pp, c_row, c_w], ds_c)
        cp("dve", c_wb.ap[:], c_w.ap[:], [c_w], [c_wb])
        act(c_row.ap[:, 1600:1664], c_row.ap[:, 1600:1664], AF.Exp, [c_row], [c_row])
        tsc("pool", c_row.ap[:, 1600:1664], c_row.ap[:, 1600:1664], -1.0, None, ALU.mult, None, [c_row], [c_row])

    PP_NMIX, PP_NMLP, PP_PSC, PP_GCW, PP_CDW, PP_CDB, PP_CLW, PP_CLB, PP_GNW = 0, 16, 32, 36, 84, 208, 212, 216, 220

    def load_x(si, l, ti):
        L = seqs[si]
        nt = L // T
        src = xs[si][l].rearrange("(c p) t -> p c t", p=128)
        t0 = ti * T
        rd = [xs_bufs[si][l][ti]]
        if ti > 0:
            rd.append(xs_bufs[si][l][ti - 1])
        if ti < nt - 1:
            rd.append(xs_bufs[si][l][ti + 1])
        if ti == 0:
            mset("pool", xw.ap[:, :, T:T + HW], 0.0, [xw])
        if ti == nt - 1:
            mset("pool", xw.ap[:, :, T + HW:W], 0.0, [xw])
        P.dma("sp", xw.ap[:, :, 0:T], src[:, :, t0:t0 + T], ds_x, rd, [xw])
        if ti > 0:
            P.dma("sp", xw.ap[:, :, T:T + HW], src[:, :, t0 - HW:t0], ds_x, rd, [xw])
        if ti < nt - 1:
            P.dma("sp", xw.ap[:, :, T + HW:W], src[:, :, t0 + T:t0 + T + HW], ds_x, rd, [xw])

    def rmsnorm_to_h(ncol, nw_off):
        for c in range(NC16):
            s = sq[c % 2]
            act(s.ap[:, 0:ncol], xw.ap[:, c, 0:ncol], AF.Square, [xw], [s])
            mm(pn.ap[:, 0:T], ones_b.ap[:], s.ap[:, 0:T], c == 0, c == NC16 - 1, [ones_b, s], [pn], inc=(ncol == T))
            if ncol > T:
                mm(pa.ap[:, 0:2 * HW], ones_b.ap[:], s.ap[:, T:W], c == 0, c == NC16 - 1, [ones_b, s], [pa])
        rsq(rstd.ap[:, 0:T], pn.ap[:, 0:T], 1.0 / D, [pn], rstd)
        if ncol > T:
            rsq(rstd.ap[:, T:W], pa.ap[:, 0:2 * HW], 1.0 / D, [pa], rstd)
        recip(rstd.ap[:, 0:ncol], rstd)
        for c in range(NC16):
            stt(h.ap[:, c, 0:ncol], xw.ap[:, c, 0:ncol], c_pp.ap[:, nw_off + c:nw_off + c + 1], rstd.ap[:, 0:ncol],
                ALU.mult, ALU.mult, [xw, c_pp, rstd], [h])

    def proj_fm(slot, jc, with_halo, evac_main, evac_halo=None):
        kv = kview(slot)
        p = nextp()
        for k in range(NC16):
            mm(p.ap[:, 0:T], kv[:, k, jc * 128:(jc + 1) * 128], h.ap[:, k, 0:T], k == 0, k == NC16 - 1, [slot, h], [p],
               inc=(k == NC16 - 1))
        evac_main(p)
        if with_halo:
            for k in range(NC16):
                mm(pn.ap[:, 0:2 * HW], kv[:, k, jc * 128:(jc + 1) * 128], h.ap[:, k, T:W], k == 0, k == NC16 - 1, [slot, h], [pn],
                   inc=(k == NC16 - 1))
            evac_halo(pn)

    def evac_window(dst, idx):
        def em(p):
            act(dst.ap[:, idx, HW:HW + T], p.ap[:, 0:T], AF.Copy, [p], [dst])

        def eh(p):
            cp("dve", dst.ap[:, idx, 0:HW], p.ap[:, 0:HW], [p], [dst])
            cp("dve", dst.ap[:, idx, HW + T:W], p.ap[:, HW:2 * HW], [p], [dst])
        return em, eh

    def gdn_stage(l, si, ti, d):
        L = seqs[si]
        base = l * NBLK
        gcw = PP_GCW
        for bi in range(3):
            slot = ws_next(base + bi)
            for jc in range(4):
                em, eh = evac_window(qkvw, jc)
                proj_fm(slot, jc, True, em, eh)
            for jc in range(4):
                idx = bi * 4 + jc
                a = tmp[jc % 2]
                for k in range(4):
                    src = qkvw.ap[:, jc, HW - 1 + k:HW - 1 + k + T]
                    wcol = c_pp.ap[:, gcw + idx * 4 + k:gcw + idx * 4 + k + 1]
                    if k == 0:
                        tsc("dve", a.ap[:], src, wcol, None, ALU.mult, None, [qkvw, c_pp], [a])
                    else:
                        stt(a.ap[:], src, wcol, a.ap[:], ALU.mult, ALU.add, [qkvw, c_pp, a], [a])
                act(qkv.ap[:, idx, :], a.ap[:], AF.Silu, [a], [qkv])
        _ck(1.1 if d == 1 else round(3.2 + (1.1 - 1) * 0.3, 4))
        for idx in range(8):
            s = sq[idx % 2]
            a = tmp[2 + idx % 2]
            act(s.ap[:, 0:T], qkv.ap[:, idx, :], AF.Square, [qkv], [s])
            mm(pn.ap[:, 0:T], ones_b.ap[:], s.ap[:, 0:T], True, True, [ones_b, s], [pn])
            rsq(a.ap[:], pn.ap[:, 0:T], 1.0, [pn], a)
            recip(a.ap[:], a)
            sc = (128.0 ** -0.5) if idx < 4 else 1.0
            stt(qkv.ap[:, idx, :], qkv.ap[:, idx, :], sc, a.ap[:], ALU.mult, ALU.mult, [qkv, a], [qkv])
            cp("pool", qkn.ap[:, idx, :], qkv.ap[:, idx, :], [qkv], [qkn])
        _ck(1.2 if d == 1 else round(3.2 + (1.2 - 1) * 0.3, 4))
        wab = c_wb.ap[:, 1024:1280].rearrange("p (k c) -> p k c", k=16)
        for c in range(8):
            for k in range(NC16):
                mm(pn.ap[0:64, c * 16:(c + 1) * 16], h.ap[:, k, c * 64:(c + 1) * 64], wab[:, k, :], k == 0, k == NC16 - 1,
                   [h, c_wb], [pn], inc=(k == NC16 - 1))
        cp("act", abt.ap[:].rearrange("p c k -> p (c k)"), pn.ap[0:64, 0:128], [pn], [abt])
        act(beta.ap[:], abt.ap[:, :, 0:8], AF.Sigmoid, [abt], [beta])
        tsc("pool", beta.ap[:], beta.ap[:], -1.0, None, ALU.mult, None, [beta], [beta])
        dtb8 = c_row.ap[0:64, 1536:1600].rearrange("p (c k) -> p c k", c=8)
        negA8 = c_row.ap[0:64, 1600:1664].rearrange("p (c k) -> p c k", c=8)
        tt("dve", gg.ap[:], abt.ap[:, :, 8:16], dtb8, ALU.add, [abt, c_row], [gg])
        act(gg.ap[:], gg.ap[:], AF.Exp, [gg], [gg])
        act(gg.ap[:], gg.ap[:], AF.Ln, [gg], [gg], bias=1.0)
        tt("dve", gg.ap[:], gg.ap[:], negA8, ALU.mult, [gg, c_row], [gg])
        _ck(1.3 if d == 1 else round(3.2 + (1.3 - 1) * 0.3, 4))
        for c in range(8):
            mm(pn.ap[0:64, 128 + c * 4:128 + c * 4 + 4], MASK(d, 0), gg.ap[:, c, d * 4:d * 4 + 4], True, True, [c_mask, gg], [pn],
               inc=(c == 7))
        _ck(1.31 if d == 1 else round(3.2 + (1.31 - 1) * 0.3, 4))
        cp("act", gcs.ap[:], pn.ap[0:64, 128:160], [pn], [gcs])
        _ck(1.32 if d == 1 else round(3.2 + (1.32 - 1) * 0.3, 4))
        for c in range(8):
            mm(pn.ap[:, 192 + c * 4:192 + c * 4 + 4], ones_f[0:64, :], gg.ap[:, c, d * 4:d * 4 + 4], True, True, [c_mask, gg], [pn],
               inc=(c == 7))
        _ck(1.33 if d == 1 else round(3.2 + (1.33 - 1) * 0.3, 4))
        cp("dve", gtot.ap[:], pn.ap[:, 192:224], [pn], [gtot])
        _ck(1.34 if d == 1 else round(3.2 + (1.34 - 1) * 0.3, 4))
        act(glast.ap[:], gtot.ap[:], AF.Exp, [gtot], [glast])
        tt("dve", dl.ap[:], gtot.ap[0:64, :], gcs.ap[:], ALU.subtract, [gtot, gcs], [dl])
        act(dl.ap[:], dl.ap[:], AF.Exp, [dl], [dl])
        _ck(1.4 if d == 1 else round(3.2 + (1.4 - 1) * 0.3, 4))
        nt = L // T
        if (d == 0 and ti == 0) or (d == 1 and ti == nt - 1):
            for hh in range(4):
                mset("pool", S32[hh].ap[:], 0.0, [S32[hh]])
                mset("pool", Sbf[hh].ap[:], 0.0, [Sbf[hh]])
        chunks = range(8) if d == 0 else range(7, -1, -1)
        for c in chunks:
            cs = slice(c * 64, (c + 1) * 64)
            H4 = range(4)
            for hh in H4:
                u = c * 4 + hh
                tsc("dve", gbc[hh].ap[:], ones_f[0:64, :], gg.ap[:, c, d * 4 + hh:d * 4 + hh + 1], None, ALU.mult, None,
                    [c_mask, gg], [gbc[hh]])
            for hh in H4:
                mm(pgA[hh].ap[:, :], gbc[hh].ap[:], MASK(d, 0), True, True, [gbc[hh], c_mask], [pgA[hh]])
            for hh in H4:
                act(grow[hh].ap[:], pgA[hh].ap[:, :], AF.Exp, [pgA[hh]], [grow[hh]])
                u = c * 4 + hh
                stt(dtm[hh].ap[:], pgA[hh].ap[0:64, :], gcs.ap[:, u:u + 1], MASK(d, 1), ALU.subtract, ALU.add,
                    [pgA[hh], gcs, c_mask], [dtm[hh]])
            _ck(1.41 if d == 1 else round(3.2 + (1.41 - 1) * 0.3, 4))
            for hh in H4:
                act(Dt[hh].ap[:], dtm[hh].ap[:], AF.Exp, [dtm[hh]], [Dt[hh]])
                tt("pool", gkb[hh].ap[:], qkv.ap[:, 4 + hh, cs], grow[hh].ap[:], ALU.mult, [qkv, grow[hh]], [gkb[hh]])
                tt("pool", gqb[hh].ap[:], qkv.ap[:, hh, cs], grow[hh].ap[:], ALU.mult, [qkv, grow[hh]], [gqb[hh]])
            _ck(1.42 if d == 1 else round(3.2 + (1.42 - 1) * 0.3, 4))
            for hh in H4:
                mm(pgB[hh].ap[0:64, 0:64], qkn.ap[:, 4 + hh, cs], qkn.ap[:, 4 + hh, cs], True, True, [qkn], [pgB[hh]], inc=False)
                mm(pgB[hh].ap[0:64, 64:128], qkn.ap[:, 4 + hh, cs], qkn.ap[:, hh, cs], True, True, [qkn], [pgB[hh]])
                tr(ptK[hh].ap[0:64, :], qkn.ap[:, 4 + hh, cs], ident_b.ap[:], [qkn, ident_b], [ptK[hh]])
                tr(pgE[hh].ap[0:64, :], qkv.ap[:, 8 + hh, cs], ident_f, [qkv, c_mask], [pgE[hh]])
            _ck(1.43 if d == 1 else round(3.2 + (1.43 - 1) * 0.3, 4))
            def ckk(n):
                _ck(n if d == 1 else round(3.2 + (n - 1) * 0.3, 5))
            for hh in H4:
                tt("dve", Dts[hh].ap[:], Dt[hh].ap[:], MASK(d, 2), ALU.mult, [Dt[hh], c_mask], [Dts[hh]])
            ckk(1.431)
            for hh in H4:
                stt(PY[hh][0].ap[:, 0:64], pgB[hh].ap[0:64, 0:64], beta.ap[:, c, d * 4 + hh:d * 4 + hh + 1], Dts[hh].ap[:],
                    ALU.mult, ALU.mult, [pgB[hh], beta, Dts[hh]], [PY[hh][0]])
            ckk(1.432)
            for hh in H4:
                tt("dve", Ptb[hh].ap[:], pgB[hh].ap[0:64, 64:128], Dt[hh].ap[:], ALU.mult, [pgB[hh], Dt[hh]], [Ptb[hh]])
            ckk(1.433)
            for hh in H4:
                cp("pool", PY[hh][0].ap[:, 64:128], ident_b.ap[0:64, 0:64], [ident_b], [PY[hh][0]])
            ckk(1.434)
            for hh in H4:
                u = c * 4 + hh
                tsc("dve", Kd[hh].ap[:], ptK[hh].ap[0:64, :], dl.ap[:, u:u + 1], None, ALU.mult, None, [ptK[hh], dl], [Kd[hh]])
            ckk(1.435)
            for hh in H4:
                cp("dve", Vt[hh].ap[:], pgE[hh].ap[0:64, :], [pgE[hh]], [Vt[hh]])
            _ck(1.44 if d == 1 else round(3.2 + (1.44 - 1) * 0.3, 4))
            for hh in H4:
                tr(ptX[hh].ap[0:64, :], PY[hh][0].ap[:, 0:64], ident_b.ap[0:64, 0:64], [PY[hh][0], ident_b], [ptX[hh]])
            for hh in H4:
                cp("act", PTt[hh][0].ap[:], ptX[hh].ap[0:64, :], [ptX[hh]], [PTt[hh][0]])
            _ck(1.5 if d == 1 else round(3.2 + (1.5 - 1) * 0.3, 4))
            for k in range(6):
                cur, nxt = k % 2, (k + 1) % 2
                for hh in H4:
                    if k < 5:
                        mm(pgC[hh].ap[0:64, :], PTt[hh][cur].ap[:], PY[hh][cur].ap[:, :], True, True,
                           [PTt[hh][cur], PY[hh][cur]], [pgC[hh]])
                        mm(pgD[hh].ap[0:64, :], PY[hh][cur].ap[:, 0:64], PTt[hh][cur].ap[:], True, True,
                           [PTt[hh][cur], PY[hh][cur]], [pgD[hh]])
                    else:
                        mm(pgC[hh].ap[0:64, 64:128], PTt[hh][cur].ap[:], PY[hh][cur].ap[:, 64:128], True, True,
                           [PTt[hh][cur], PY[hh][cur]], [pgC[hh]])
                for hh in H4:
                    if k < 5:
                        cp("act", PY[hh][nxt].ap[:, 0:64], pgC[hh].ap[0:64, 0:64], [pgC[hh]], [PY[hh][nxt]])
                        cp("act", PTt[hh][nxt].ap[:], pgD[hh].ap[0:64, :], [pgD[hh]], [PTt[hh][nxt]])
                    tt("dve", PY[hh][nxt].ap[:, 64:128], pgC[hh].ap[0:64, 64:128], PY[hh][cur].ap[:, 64:128], ALU.add,
                       [pgC[hh], PY[hh][cur]], [PY[hh][nxt]])
            YF = 0
            _ck(1.6 if d == 1 else round(3.2 + (1.6 - 1) * 0.3, 4))
            for hh in H4:
                mm(pgB[hh].ap[0:64, :], gkb[hh].ap[:], Sbf[hh].ap[:], True, True, [gkb[hh], Sbf[hh]], [pgB[hh]])
            for hh in H4:
                tt("dve", Rp[hh].ap[:], pgB[hh].ap[0:64, :], Vt[hh].ap[:], ALU.subtract, [Vt[hh], pgB[hh]], [Rp[hh]])
            for hh in H4:
                mm(pgB[hh].ap[0:64, :], PY[hh][YF].ap[:, 64:128], Rp[hh].ap[:], True, True, [PY[hh][YF], Rp[hh]], [pgB[hh]])
            for hh in H4:
                tsc("dve", Vn[hh].ap[:], pgB[hh].ap[0:64, :], beta.ap[:, c, d * 4 + hh:d * 4 + hh + 1], None, ALU.mult, None,
                    [pgB[hh], beta], [Vn[hh]])
            for hh in H4:
                mm(pgA[hh].ap[:, :], Sbf[hh].ap[:], gqb[hh].ap[:], True, True, [Sbf[hh], gqb[hh]], [pgA[hh]])
                mm(pgD[hh].ap[:, :], Vn[hh].ap[:], Ptb[hh].ap[:], True, True, [Vn[hh], Ptb[hh]], [pgD[hh]])
                mm(pgC[hh].ap[:, :], Kd[hh].ap[:], Vn[hh].ap[:], True, True, [Kd[hh], Vn[hh]], [pgC[hh]])
            for hh in H4:
                u = c * 4 + hh
                if d == 1:
                    cp("act", obT.ap[:, hh, cs], pgA[hh].ap[:, :], [pgA[hh]], [obT])
                else:
                    tt("dve", obT.ap[:, hh, cs], pgA[hh].ap[:, :], obT.ap[:, hh, cs], ALU.add, [pgA[hh], obT], [obT])
                tt("dve", obT.ap[:, hh, cs], pgD[hh].ap[:, :], obT.ap[:, hh, cs], ALU.add, [pgD[hh], obT], [obT])
                stt(S32[hh].ap[:], S32[hh].ap[:], glast.ap[:, u:u + 1], pgC[hh].ap[:, :], ALU.mult, ALU.add,
                    [S32[hh], glast, pgC[hh]], [S32[hh]])
                cp("act", Sbf[hh].ap[:], S32[hh].ap[:], [S32[hh]], [Sbf[hh]])

    def ob_dram(si, ti):
        return obs[si].rearrange("(c p) t -> p c t", p=128)[:, :, ti * T:(ti + 1) * T]

    def gdn_finish(l, si, ti):
        slot = ws_next(l * NBLK + 3)
        for jc in range(4):
            def em(p, jc=jc):
                act(zs.ap[:, jc, :], p.ap[:, 0:T], AF.Silu, [p], [zs])
            proj_fm(slot, jc, False, em)
        for hh in range(4):
            s = sq[hh % 2]
            a = tmp[hh % 2]
            act(s.ap[:, 0:T], obT.ap[:, hh, :], AF.Square, [obT], [s])
            mm(pn.ap[:, 0:T], ones_b.ap[:], s.ap[:, 0:T], True, True, [ones_b, s], [pn])
            rsq(a.ap[:], pn.ap[:, 0:T], 1.0 / 128, [pn], a)
            recip(a.ap[:], a)
            stt(a.ap[:], obT.ap[:, hh, :], c_pp.ap[:, PP_GNW:PP_GNW + 1], a.ap[:], ALU.mult, ALU.mult, [obT, c_pp, a], [a])
            tt("pool", binb[1].ap[:, hh, :], a.ap[:], zs.ap[:, hh, :], ALU.mult, [a, zs], [binb[1]])

    def pool_stage(l, si, ti):
        L = seqs[si]
        nt = L // T
        slot = ws_next(l * NBLK + 4)
        for g in range(4):
            em, eh = evac_window(bsA, g)
            proj_fm(slot, g, True, em, eh)
        pw = c_wb.ap[:, 0:512].rearrange("p (g c) -> p g c", g=4)
        for g in range(4):
            w = POOL_WINDOWS[g]
            src = bsA.ap[:, g, :]
            cur_ap, cur_b, n = src, bsA, W
            step = 1
            i = 0
            while step < w:
                dst = stmp[i % 2]
                n2 = n - step
                tt("pool", dst.ap[:, 0:n2], cur_ap[:, 0:n2], cur_ap[:, step:step + n2], ALU.add, [cur_b], [dst])
                cur_ap, cur_b, n = dst.ap, dst, n2
                step *= 2
                i += 1
            off = HW - w // 2
            stt(dbf.ap[:], cur_ap[:, off:off + T], 1.0 / w, bsA.ap[:, g, HW:HW + T], ALU.mult, ALU.subtract, [cur_b, bsA], [dbf])
            if ti == 0:
                a = tmp[0]
                tt("dve", a.ap[:, 0:8], cur_ap[:, off:off + 8], c_mask.ap[:, 512 + g * 8:512 + g * 8 + 8], ALU.mult, [cur_b, c_mask], [a])
                tt("dve", dbf.ap[:, 0:8], a.ap[:, 0:8], bsA.ap[:, g, HW:HW + 8], ALU.subtract, [a, bsA], [dbf])
            if ti == nt - 1:
                a = tmp[0]
                tt("dve", a.ap[:, 0:8], cur_ap[:, off + T - 8:off + T], c_mask.ap[:, 576 + g * 8:576 + g * 8 + 8], ALU.mult,
                   [cur_b, c_mask], [a])
                tt("dve", dbf.ap[:, T - 8:T], a.ap[:, 0:8], bsA.ap[:, g, HW + T - 8:HW + T], ALU.subtract, [a, bsA], [dbf])
            p = nextp()
            mm(p.ap[:, 0:T], pw[:, g, :], dbf.ap[:], True, True, [c_wb, dbf], [p])
            tsc("dve", binb[0].ap[:, g, :], p.ap[:, 0:T], c_pp.ap[:, PP_PSC + g:PP_PSC + g + 1], None, ALU.mult, None, [p, c_pp], [binb[0]])

    def conf_stage(l, si, ti):
        slot = ws_next(l * NBLK + 5)
        for jc in range(4):
            em, eh = evac_window(bsA, jc)
            proj_fm(slot, jc, True, em, eh)
        slot = ws_next(l * NBLK + 6)
        for jc in range(4):
            def em(p, jc=jc):
                act(bsB.ap[:, jc, HW:HW + T], p.ap[:, 0:T], AF.Sigmoid, [p], [bsB])

            def eh(p, jc=jc):
                act(bsB.ap[:, jc, 0:HW], p.ap[:, 0:HW], AF.Sigmoid, [p], [bsB])
                act(bsB.ap[:, jc, HW + T:W], p.ap[:, HW:2 * HW], AF.Sigmoid, [p], [bsB])
            proj_fm(slot, jc, True, em, eh)
        for jc in range(4):
            tt("pool", bsA.ap[:, jc, :], bsA.ap[:, jc, :], bsB.ap[:, jc, :], ALU.mult, [bsA, bsB], [bsA])
        for jc in range(4):
            for k in range(31):
                src = bsA.ap[:, jc, 1 + k:1 + k + T]
                wcol = c_pp.ap[:, PP_CDW + jc * 31 + k:PP_CDW + jc * 31 + k + 1]
                if k == 0:
                    tsc("dve", bsB.ap[:, jc, 0:T], src, wcol, c_pp.ap[:, PP_CDB + jc:PP_CDB + jc + 1], ALU.mult, ALU.add,
                        [bsA, c_pp], [bsB])
                else:
                    stt(bsB.ap[:, jc, 0:T], src, wcol, bsB.ap[:, jc, 0:T], ALU.mult, ALU.add, [bsA, c_pp, bsB], [bsB])
        for jc in range(4):
            s = sq[jc % 2]
            cp("act", s.ap[:, 0:T], bsB.ap[:, jc, 0:T], [bsB], [s])
            mm(pn.ap[:, 0:T], ones_b.ap[:], s.ap[:, 0:T], jc == 0, jc == 3, [ones_b, s], [pn])
        mean = tmp[0]
        tsc("dve", mean.ap[:], pn.ap[:, 0:T], 1.0 / 512, None, ALU.mult, None, [pn], [mean])
        for jc in range(4):
            s = sq[jc % 2]
            a = tmp[2 + jc % 2]
            tt("dve", bsB.ap[:, jc, 0:T], bsB.ap[:, jc, 0:T], mean.ap[:], ALU.subtract, [bsB, mean], [bsB])
            act(s.ap[:, 0:T], bsB.ap[:, jc, 0:T], AF.Square, [bsB], [s])
            mm(pn.ap[:, 0:T], ones_b.ap[:], s.ap[:, 0:T], jc == 0, jc == 3, [ones_b, s], [pn])
        rs = tmp[1]
        rsq(rs.ap[:], pn.ap[:, 0:T], 1.0 / 512, [pn], rs)
        recip(rs.ap[:], rs)
        for jc in range(4):
            a = tmp[2 + jc % 2]
            tt("dve", a.ap[:], bsB.ap[:, jc, 0:T], rs.ap[:], ALU.mult, [bsB, rs], [a])
            act(binb[2].ap[:, jc, :], a.ap[:], AF.Silu, [a, c_pp], [binb[2]],
                scale=c_pp.ap[:, PP_CLW + jc:PP_CLW + jc + 1], bias=c_pp.ap[:, PP_CLB + jc:PP_CLB + jc + 1])

    def sgu_stage(l, si, ti):
        slot = ws_next(l * NBLK + 7)
        for jc in range(4):
            def em(p, jc=jc):
                act(bsA.ap[:, jc, 0:T], p.ap[:, 0:T], AF.Gelu_apprx_tanh, [p], [bsA])
            proj_fm(slot, jc, False, em)
        slot = ws_next(l * NBLK + 8)
        kv = kview(slot)
        swT = c_wb.ap[:, 512:1024].rearrange("p (g c) -> p g c", g=4)
        sgb = c_row.ap[:, 1024:1536].rearrange("p (g c) -> p g c", g=4)
        for tb in range(4):
            ts_ = slice(tb * 128, (tb + 1) * 128)
            p = nextp()
            for k in range(NC16):
                mm(p.ap[:, 0:T], h.ap[:, k, ts_], kv[:, k, :], k == 0, k == NC16 - 1, [slot, h], [p], inc=(k == NC16 - 1))
            act(vg.ap[:], p.ap[:, 0:T], AF.Gelu_apprx_tanh, [p], [vg])
            act(sqv.ap[:], vg.ap[:], AF.Square, [vg], [sqv])
            P.op("dve", lambda e: e.tensor_reduce(st4.ap[:, 0:1], vg.ap[:], mybir.AxisListType.X, ALU.add), [vg], [st4])
            P.op("dve", lambda e: e.tensor_reduce(st4.ap[:, 1:2], sqv.ap[:], mybir.AxisListType.X, ALU.add), [sqv], [st4])
            tsc("dve", st4.ap[:, 0:2], st4.ap[:, 0:2], 1.0 / 512, None, ALU.mult, None, [st4], [st4])
            tt("dve", st4.ap[:, 2:3], st4.ap[:, 0:1], st4.ap[:, 0:1], ALU.mult, [st4], [st4])
            tt("dve", st4.ap[:, 3:4], st4.ap[:, 1:2], st4.ap[:, 2:3], ALU.subtract, [st4], [st4])
            rsq(st4.ap[:, 3:4], st4.ap[:, 3:4], 1.0, [st4], st4)
            recip(st4.ap[:, 3:4], st4)
            tsc("dve", vg.ap[:], vg.ap[:], st4.ap[:, 0:1], st4.ap[:, 3:4], ALU.subtract, ALU.mult, [vg, st4], [vg])
            tt("pool", vg.ap[:], vg.ap[:], c_row.ap[:, 0:512], ALU.mult, [vg, c_row], [vg])
            vl = vln[tb % 2]
            tt("pool", vl.ap[:], vg.ap[:], c_row.ap[:, 512:1024], ALU.add, [vg, c_row], [vl])
            for g in range(4):
                mm(pn.ap[:, g * 128:(g + 1) * 128], vl.ap[:, g * 128:(g + 1) * 128], swT[:, g, :], True, True, [vl, c_wb], [pn],
                   inc=(g == 3))
            a = tmp[tb % 2]
            tt("dve", a.ap[:].rearrange("p (g c) -> p g c", g=4), pn.ap[:, 0:T].rearrange("p (g c) -> p g c", g=4), sgb, ALU.add,
               [pn, c_row], [a])
            tt("pool", binb[3].ap[:, :, ts_], a.ap[:].rearrange("p (g c) -> p g c", g=4), bsA.ap[:, :, tb * 128:(tb + 1) * 128],
               ALU.mult, [a, bsA], [binb[3]])

    class WPS:
        order = []
        pos = 0
        issued = 0

    def wp_next(bid_check):
        while WPS.issued < min(WPS.pos + 2, len(WPS.order)):
            gb = WPS.order[WPS.issued]
            s = WPS.issued % 2
            P.dma("sp", psl[s].ap[:, :], wpbf[gb], ds_p[s], [wp_bufs[gb]], [psl[s]])
            WPS.issued += 1
        assert WPS.order[WPS.pos] == bid_check
        s = WPS.pos % 2
        WPS.pos += 1
        return psl[s]

    def merge_stage(l):
        base = l * NBLK
        for mq in range(4):
            for b in range(4):
                gslot = ws_next(base + 9 + mq * 4 + b)
                pslot = wp_next(l * 16 + mq * 4 + b)
                gv = kview(gslot)
                pv = pslot.ap[:, :].rearrange("p (k c) -> p k c", k=4)
                for mi in range(4):
                    m = mq * 4 + mi
                    p = nextp()
                    for k in range(NC16):
                        mm(p.ap[:, 0:T], gv[:, k, mi * 128:(mi + 1) * 128], h.ap[:, k, 0:T], k == 0, k == NC16 - 1, [gslot, h], [p],
                           inc=(k == NC16 - 1))
                    gsb = tmp[mi % 2]
                    act(gsb.ap[:], p.ap[:, 0:T], AF.Sigmoid, [p], [gsb])
                    for k in range(4):
                        mm(pn.ap[:, 0:T], pv[:, k, mi * 128:(mi + 1) * 128], binb[b].ap[:, k, :], k == 0, k == 3, [pslot, binb[b]], [pn],
                           inc=(k == 3))
                    acc = accb[mi]
                    if b == 0:
                        tt("dve", acc.ap[:], gsb.ap[:], pn.ap[:, 0:T], ALU.mult, [gsb, pn], [acc])
                    else:
                        t2 = tmp[2 + mi % 2]
                        tt("dve", t2.ap[:], gsb.ap[:], pn.ap[:, 0:T], ALU.mult, [gsb, pn], [t2])
                        if b < 3:
                            tt("pool", acc.ap[:], acc.ap[:], t2.ap[:], ALU.add, [acc, t2], [acc])
                        else:
                            tt("pool", mbf.ap[:, m, :], acc.ap[:], t2.ap[:], ALU.add, [acc, t2], [mbf])

    def wout_stage(l):
        base = l * NBLK
        for mq in range(4):
            slot = ws_next(base + 25 + mq)
            kv = kview(slot)
            for mi in range(4):
                m = mq * 4 + mi
                p = nextp()
                for k in range(NC16):
                    mm(p.ap[:, 0:T], kv[:, k, mi * 128:(mi + 1) * 128], mbf.ap[:, k, :], k == 0, k == NC16 - 1, [slot, mbf], [p],
                       inc=(k == NC16 - 1))
                tt("dve", xw.ap[:, m, 0:T], p.ap[:, 0:T], xw.ap[:, m, 0:T], ALU.add, [p, xw], [xw])

    def mlp_stage(l):
        base = l * NBLK
        for fq in range(4):
            u = uq[fq % 2]
            for fi4 in range(4):
                slot = ws_next(base + 29 + fq * 8 + fi4)
                kv = kview(slot)
                for fi in range(4):
                    fc = fi4 * 4 + fi
                    p = nextp()
                    for k in range(NC16):
                        mm(p.ap[:, 0:T], kv[:, k, fi * 128:(fi + 1) * 128], h.ap[:, k, 0:T], k == 0, k == NC16 - 1, [slot, h], [p],
                           inc=(k == NC16 - 1))
                    r = tmp[fi % 2]
                    act(r.ap[:], p.ap[:, 0:T], AF.Relu, [p], [r])
                    tt("pool", u.ap[:, fc, :], r.ap[:], r.ap[:], ALU.mult, [r], [u])
            for mq in range(4):
                slot = ws_next(base + 29 + fq * 8 + 4 + mq)
                kv = kview(slot)
                for mi in range(4):
                    m = mq * 4 + mi
                    p = nextp()
                    for k in range(NC16):
                        mm(p.ap[:, 0:T], kv[:, k, mi * 128:(mi + 1) * 128], u.ap[:, k, :], k == 0, k == NC16 - 1, [slot, u], [p],
                           inc=(k == NC16 - 1))
                    tt("dve", xw.ap[:, m, 0:T], p.ap[:, 0:T], xw.ap[:, m, 0:T], ALU.add, [p, xw], [xw])

    def final_norm_store(si, ti):
        for c in range(NC16):
            s = sq[c % 2]
            act(s.ap[:, 0:T], xw.ap[:, c, 0:T], AF.Square, [xw], [s])
            mm(pn.ap[:, 0:T], ones_b.ap[:], s.ap[:, 0:T], c == 0, c == NC16 - 1, [ones_b, s], [pn])
        rsq(rstd.ap[:, 0:T], pn.ap[:, 0:T], 1.0 / D, [pn], rstd)
        recip(rstd.ap[:, 0:T], rstd)
        for c in range(NC16):
            stt(yst.ap[:, c, :], xw.ap[:, c, 0:T], c_fin.ap[:, c:c + 1], rstd.ap[:, 0:T], ALU.mult, ALU.mult, [xw, c_fin, rstd], [yst])
        dst = yout[si].rearrange("(c p) t -> p c t", p=128)[:, :, ti * T:(ti + 1) * T]
        P.dma("sp", dst, yst.ap[:], ds_y, [yst], [])

    def order_pass1(l):
        return [l * NBLK + b for b in (0, 1, 2)]

    def order_pass2(l):
        o = [0, 1, 2, 3, 4, 5, 6, 7, 8] + list(range(9, 25)) + list(range(25, 29)) + list(range(29, 61))
        return [l * NBLK + b for b in o]

    for l in range(depth):
        for si, L in enumerate(seqs):
            nt = L // T
            for ti in range(nt):
                WS.order += order_pass1(l)
            for ti in range(nt):
                WS.order += order_pass2(l)
                WPS.order += [l * 16 + i for i in range(16)]

    try:
      _ck(0)
      for l in range(depth):
          P.fence()
          load_layer_consts(l)
          P.fence()
          for si, L in enumerate(seqs):
              nt = L // T
              for ti in range(nt - 1, -1, -1):
                  load_x(si, l, ti)
                  rmsnorm_to_h(W, PP_NMIX)
                  _ck(1)
                  gdn_stage(l, si, ti, 1)
                  _ck(2)
                  P.dma("sp", ob_dram(si, ti), obT.ap[:], ds_ob, [obT], [ob_bufs[si][ti]])
                  P.fence()
              _ck(3)
              for ti in range(nt):
                  load_x(si, l, ti)
                  rmsnorm_to_h(W, PP_NMIX)
                  P.dma("sp", obT.ap[:], ob_dram(si, ti), ds_ob, [ob_bufs[si][ti]], [obT])
                  _ck(3.2)
                  gdn_stage(l, si, ti, 0)
                  _ck(3.5)
                  gdn_finish(l, si, ti)
                  P.fence()
                  _ck(4)
                  pool_stage(l, si, ti)
                  _ck(5)
                  conf_stage(l, si, ti)
                  _ck(6)
                  sgu_stage(l, si, ti)
                  P.fence()
                  _ck(7)
                  merge_stage(l)
                  _ck(8)
                  wout_stage(l)
                  P.fence()
                  _ck(9)
                  rmsnorm_to_h(T, PP_NMLP)
                  mlp_stage(l)
                  P.fence()
                  _ck(10)
                  if l < depth - 1:
                      dst = xs[si][l + 1].rearrange("(c p) t -> p c t", p=128)[:, :, ti * T:(ti + 1) * T]
                      P.dma("sp", dst, xw.ap[:, :, 0:T], ds_x, [xw], [xs_bufs[si][l + 1][ti]])
                  else:
                      final_norm_store(si, ti)
                      P.fence()
    except _Stop:
        pass
    P.wait_ev("sp", (ds_y.sem, ds_y.cnt))
    P.wait_ev("sp", (ds_x.sem, ds_x.cnt))
    P.wait_ev("sp", (ds_ob.sem, ds_ob.cnt))
    for dsx in [ds_w[0], ds_w[1], ds_p[0], ds_p[1], ds_c] + ds_cast:
        if dsx.cnt > 0:
            P.wait_ev("sp", (dsx.sem, dsx.cnt))
    for e in ("pe", "act", "dve", "pool"):
        if P.cnt[e] > 0:
            P.wait_ev("sp", (P.sem[e], P.cnt[e]))
    assert STOP[0] < 99 or (WS.pos == len(WS.order) and WPS.pos == len(WPS.order))

    with nc.Block() as block:
        @block.tensor
        def _(e):
            for f in P.streams["pe"]:
                f(e)

        @block.scalar
        def _(e):
            for f in P.streams["act"]:
                f(e)

        @block.vector
        def _(e):
            for f in P.streams["dve"]:
                f(e)

        @block.gpsimd
        def _(e):
            for f in P.streams["pool"]:
                f(e)

        @block.sync
        def _(e):
            for f in P.streams["sp"]:
                f(e)
    stack.close()
    return nc, P


def _kblk(Wsub):
    K, C = Wsub.shape
    kc = K // 128
    return np.ascontiguousarray(Wsub.reshape(kc, 128, C).transpose(1, 0, 2).reshape(128, kc * C))


def _pp(v):
    return np.ascontiguousarray(v.reshape(-1, 128).T)


def build_host_arrays(inp, depth, seqL_for_tables):
    f = np.float32
    wb = np.empty((depth * NBLK, 128, 8192), f)
    wp = np.empty((depth * 16, 128, 2048), f)
    cpp = np.zeros((depth, 128, 512), f)
    crow = np.zeros((depth, 128, 2048), f)
    cw = np.zeros((depth, 128, 1280), f)
    projs = ["pool_proj", "gdn_proj", "conf_proj", "sgu_proj"]
    for l in range(depth):
        w_in = inp["w_in"][l]
        B = l * NBLK
        cols = {0: (512, 1024), 1: (1024, 1536), 2: (1536, 2048), 3: (2048, 2560), 4: (0, 512), 5: (2576, 3088), 6: (3088, 3600),
                7: (3600, 4112), 8: (4112, 4624)}
        for b, (a, e) in cols.items():
            wb[B + b] = _kblk(w_in[:, a:e])
        for mq in range(4):
            for b in range(4):
                a = 4624 + b * 2048 + mq * 512
                wb[B + 9 + mq * 4 + b] = _kblk(w_in[:, a:a + 512])
                wp[l * 16 + mq * 4 + b] = _kblk(inp[projs[b]][l][:, mq * 512:(mq + 1) * 512])
        for mq in range(4):
            wb[B + 25 + mq] = _kblk(inp["w_out"][l][:, mq * 512:(mq + 1) * 512])
        for fq in range(4):
            for fi4 in range(4):
                a = fq * 2048 + fi4 * 512
                wb[B + 29 + fq * 8 + fi4] = _kblk(inp["mlp_w1"][l][:, a:a + 512])
            for mq in range(4):
                wb[B + 29 + fq * 8 + 4 + mq] = _kblk(inp["mlp_w2"][l][fq * 2048:(fq + 1) * 2048, mq * 512:(mq + 1) * 512])
        cpp[l, :, 0:16] = _pp(inp["norm_mix_w"][l])
        cpp[l, :, 16:32] = _pp(inp["norm_mlp_w"][l])
        cpp[l, :, 32:36] = _pp(inp["pool_scale"][l])
        gc = inp["gdn_conv_w"][l]
        cpp[l, :, 36:84] = gc.T.reshape(12, 128, 4).transpose(1, 0, 2).reshape(128, 48)
        cd = inp["conf_dw_w"][l]
        cpp[l, :, 84:208] = cd.T.reshape(4, 128, 31).transpose(1, 0, 2).reshape(128, 124)
        cpp[l, :, 208:212] = _pp(inp["conf_dw_b"][l])
        cpp[l, :, 212:216] = _pp(inp["conf_ln_w"][l])
        cpp[l, :, 216:220] = _pp(inp["conf_ln_b"][l])
        cpp[l, :, 220] = inp["gdn_norm_w"][l]
        crow[l, :, 0:512] = inp["sgu_ln_w"][l][None, :]
        crow[l, :, 512:1024] = inp["sgu_ln_b"][l][None, :]
        crow[l, :, 1024:1536] = inp["sgu_b"][l].reshape(1, 512)
        crow[l, :, 1536:1600] = np.tile(inp["gdn_dt_bias"][l].reshape(1, 8), (1, 8))
        crow[l, :, 1600:1664] = np.tile(inp["gdn_a_log"][l].reshape(1, 8), (1, 8))
        cw[l, :, 0:512] = inp["pool_w"][l].transpose(1, 0, 2).reshape(128, 512)
        cw[l, :, 512:1024] = inp["sgu_w"][l].transpose(2, 0, 1).reshape(128, 512)
        cw[l, :, 1024:1280] = _kblk(w_in[:, 2560:2576])
    cm = np.zeros((128, 1024), f)
    cm[:, 0:128] = np.eye(128, dtype=f)
    j = np.arange(64)[:, None]
    i = np.arange(64)[None, :]
    for d in range(2):
        le = (j <= i) if d == 0 else (j >= i)
        lt = (j < i) if d == 0 else (j > i)
        cm[0:64, 128 + (d * 3 + 0) * 64:128 + (d * 3 + 0) * 64 + 64] = le
        cm[0:64, 128 + (d * 3 + 1) * 64:128 + (d * 3 + 1) * 64 + 64] = np.where(le, 0.0, NEG)
        cm[0:64, 128 + (d * 3 + 2) * 64:128 + (d * 3 + 2) * 64 + 64] = lt
    for g, w in enumerate(POOL_WINDOWS):
        for t in range(8):
            cnt = t + w // 2 if t < w // 2 else w
            cm[:, 512 + g * 8 + t] = 1.0 / cnt
            r = 8 - t
            cnt = (r + w // 2) if (w - w // 2) > r else w
            cm[:, 576 + g * 8 + t] = 1.0 / cnt
    cm[:, 896:1024] = 1.0
    cfin = _pp(inp["norm_final_w"])
    return wb, wp, cpp, crow, cw, cm, cfin


_PROG_CACHE = {}


def run_config(seq_inputs, params, depth, n_cores):
    seqs = [a[0].shape[0] for a in seq_inputs]
    wb, wp, cpp, crow, cw, cm, cfin = build_host_arrays(params, depth, seqs)
    key = (tuple(seqs), depth)
    if key not in _PROG_CACHE:
        _PROG_CACHE[key] = build_program(seqs, depth)
    nc, P = _PROG_CACHE[key]
    in_maps = []
    for c in range(n_cores):
        m = {"wb": wb, "wp": wp, "cpp": cpp, "crow": crow, "cw": cw, "cmask": cm, "cfin": cfin}
        for i, a in enumerate(seq_inputs):
            m[f"x{i}"] = np.ascontiguousarray(a[c].T)
        in_maps.append(m)
    res = run_bass_kernel_spmd(nc, in_maps, core_ids=list(range(n_cores)))
    outs = []
    for i in range(len(seqs)):
        outs.append([np.ascontiguousarray(res.results[c][f"y{i}"].T) for c in range(n_cores)])
    return outs


def kernel(**inputs):
    inp = {k: np.asarray(v) for k, v in inputs.items()}
    xp = inp["x_prompt"]
    xsamp = inp["x_sample"]
    depth = inp["w_in"].shape[0]
    n = 8
    outs = run_config([[xp[c] for c in range(n)], [xsamp[0] for _ in range(n)]], inp, depth, n)
    y_prompt = np.stack(outs[0], axis=0).astype(np.float32)
    y_sample = outs[1][0][None].astype(np.float32)
    return (y_prompt, y_sample)
```
